# Optimizing a Trainium2 kernel written in Bass

```python
import math
import jax, jax.numpy as jnp
from jax import lax
import numpy as np

D_MODEL = 1024
BATCH = 16
SEQ = 2048
DEPTH = 2

N_A = max(1, DEPTH // 2)
N_B = DEPTH - N_A

D_RNN = 1280
RNN_BLOCKS = 10
RNN_BW = D_RNN // RNN_BLOCKS
CONV_WIDTH = 4
LRU_C = 8.0

N_HEADS = 8
QK_NOPE = 128
QK_ROPE = 64
V_DIM = 128
KV_RANK = 256
Q_RANK = 384
ROPE_THETA = 10000.0
Q_BLOCK = 128
ATTN_SCALE = (QK_NOPE + QK_ROPE) ** -0.5
EPS = 1e-6

kernel_name = "yoco_rglru_mla_hybrid"


def rms_norm(x, g):
    xf = x.astype(jnp.float32)
    y = xf * lax.rsqrt(jnp.mean(xf * xf, axis=-1, keepdims=True) + EPS)
    return (y * g.astype(jnp.float32)).astype(x.dtype)


def rope_tables(seq_len):
    pos = jnp.arange(seq_len, dtype=jnp.float32)
    inv = ROPE_THETA ** (-jnp.arange(0, QK_ROPE, 2, dtype=jnp.float32) / QK_ROPE)
    ang = pos[:, None] * inv[None, :]
    return jnp.cos(ang), jnp.sin(ang)


def apply_rope(x, cos, sin):
    xf = x.astype(jnp.float32)
    x1, x2 = jnp.split(xf, 2, axis=-1)
    out = jnp.concatenate([x1 * cos - x2 * sin, x2 * cos + x1 * sin], axis=-1)
    return out.astype(x.dtype)


def causal_depthwise_conv(x, w, b):
    c = x.shape[-1]
    y = lax.conv_general_dilated(
        x, w[:, None, :].astype(x.dtype), window_strides=(1,),
        padding=[(CONV_WIDTH - 1, 0)], dimension_numbers=("NWC", "WIO", "NWC"),
        feature_group_count=c)
    return y + b


def _lin_combine(left, right):
    a1, b1 = left
    a2, b2 = right
    return a1 * a2, a2 * b1 + b2


def rglru_layer(x, norm_g, w_in, conv_w, conv_b, w_rg, b_rg, w_ig, b_ig, lam, w_out):
    bsz, seq, _ = x.shape
    h = rms_norm(x, norm_g)
    u = h @ w_in
    xb, gate = u[..., :D_RNN], u[..., D_RNN:]
    xb = causal_depthwise_conv(xb, conv_w, conv_b)
    xblk = xb.reshape(bsz, seq, RNN_BLOCKS, RNN_BW)
    r = jax.nn.sigmoid(jnp.einsum("bsnc,ncd->bsnd", xblk, w_rg).reshape(bsz, seq, D_RNN) + b_rg)
    i = jax.nn.sigmoid(jnp.einsum("bsnc,ncd->bsnd", xblk, w_ig).reshape(bsz, seq, D_RNN) + b_ig)
    log_a = -LRU_C * r.astype(jnp.float32) * jax.nn.softplus(-lam.astype(jnp.float32))
    a = jnp.exp(log_a)
    bterm = jnp.sqrt(-jnp.expm1(2.0 * log_a)) * (i * xb).astype(jnp.float32)
    _, hs = lax.associative_scan(_lin_combine, (a, bterm), axis=1)
    y = hs.astype(x.dtype) * jax.nn.silu(gate)
    return y @ w_out


def mla_shared_kv(x_stream, norm_kv, w_dkv, kv_norm, w_uk, w_uv, cos, sin):
    h = rms_norm(x_stream, norm_kv)
    ckr = h @ w_dkv
    c_kv = rms_norm(ckr[..., :KV_RANK], kv_norm)
    k_rope = apply_rope(ckr[..., KV_RANK:], cos[None], sin[None])
    k_nope = jnp.einsum("bsc,chd->bshd", c_kv, w_uk)
    v = jnp.einsum("bsc,chd->bshd", c_kv, w_uv)
    return k_nope, k_rope, v


def causal_block_attention(q_nope, q_rope, k_nope, k_rope, v):
    bsz, seq, nh, _ = q_nope.shape
    nb = seq // Q_BLOCK
    qn = q_nope.reshape(bsz, nb, Q_BLOCK, nh, QK_NOPE).transpose(1, 0, 2, 3, 4)
    qr = q_rope.reshape(bsz, nb, Q_BLOCK, nh, QK_ROPE).transpose(1, 0, 2, 3, 4)
    kpos = jnp.arange(seq)

    def one_block(args):
        qn_b, qr_b, bi = args
        s = jnp.einsum("bqhd,bkhd->bhqk", qn_b, k_nope, preferred_element_type=jnp.float32)
        s = s + jnp.einsum("bqhr,bkr->bhqk", qr_b, k_rope, preferred_element_type=jnp.float32)
        s = s * ATTN_SCALE
        qpos = bi * Q_BLOCK + jnp.arange(Q_BLOCK)
        mask = kpos[None, :] <= qpos[:, None]
        s = jnp.where(mask[None, None], s, -jnp.inf)
        p = jax.nn.softmax(s, axis=-1)
        return jnp.einsum("bhqk,bkhd->bqhd", p.astype(v.dtype), v)

    o = lax.map(one_block, (qn, qr, jnp.arange(nb)))
    return o.transpose(1, 0, 2, 3, 4).reshape(bsz, seq, nh, V_DIM)


def mla_layer(x, norm_g, w_in, q_norm, w_uq, w_out, k_nope, k_rope, v, cos, sin):
    bsz, seq, _ = x.shape
    h = rms_norm(x, norm_g)
    u = h @ w_in
    c_q = rms_norm(u[..., :Q_RANK], q_norm)
    gate = u[..., Q_RANK:]
    q = jnp.einsum("bsc,chd->bshd", c_q, w_uq)
    q_nope = q[..., :QK_NOPE]
    q_rope = apply_rope(q[..., QK_NOPE:], cos[None, :, None], sin[None, :, None])
    o = causal_block_attention(q_nope, q_rope, k_nope, k_rope, v)
    y = o.reshape(bsz, seq, N_HEADS * V_DIM) * jax.nn.silu(gate)
    return y @ w_out


def setup_inputs(seed: int = 0) -> dict:
    key = jax.random.key(seed)
    ks = jax.random.split(key, 24)
    f32 = jnp.float32
    nrm = lambda k, shape, fan_in: jax.random.normal(k, shape, f32) * (fan_in ** -0.5)
    gain = lambda k, shape: 1.0 + 0.02 * jax.random.normal(k, shape, f32)
    small = lambda k, shape: 0.01 * jax.random.normal(k, shape, f32)

    u = jax.random.uniform(ks[8], (N_A, D_RNN), f32, 0.9, 0.999)
    a0 = u ** (1.0 / LRU_C)
    lam = jnp.log(a0) - jnp.log1p(-a0)

    return {
        "x": jax.random.normal(ks[0], (BATCH, SEQ, D_MODEL), f32),
        "norm_a": gain(ks[1], (N_A, D_MODEL)),
        "w_in_a": nrm(ks[2], (N_A, D_MODEL, 2 * D_RNN), D_MODEL),
        "conv_w": nrm(ks[3], (N_A, CONV_WIDTH, D_RNN), CONV_WIDTH),
        "conv_b": small(ks[4], (N_A, D_RNN)),
        "w_rg": nrm(ks[5], (N_A, RNN_BLOCKS, RNN_BW, RNN_BW), RNN_BW),
        "b_rg": small(ks[6], (N_A, D_RNN)),
        "w_ig": nrm(ks[7], (N_A, RNN_BLOCKS, RNN_BW, RNN_BW), RNN_BW),
        "b_ig": small(ks[9], (N_A, D_RNN)),
        "lru_lambda": lam,
        "w_out_a": nrm(ks[10], (N_A, D_RNN, D_MODEL), D_RNN),
        "norm_kv": gain(ks[11], (D_MODEL,)),
        "w_dkv": nrm(ks[12], (D_MODEL, KV_RANK + QK_ROPE), D_MODEL),
        "kv_norm": gain(ks[13], (KV_RANK,)),
        "w_uk": nrm(ks[14], (KV_RANK, N_HEADS, QK_NOPE), KV_RANK),
        "w_uv": nrm(ks[15], (KV_RANK, N_HEADS, V_DIM), KV_RANK),
        "norm_b": gain(ks[16], (N_B, D_MODEL)),
        "w_in_b": nrm(ks[17], (N_B, D_MODEL, Q_RANK + N_HEADS * V_DIM), D_MODEL),
        "q_norm": gain(ks[18], (N_B, Q_RANK)),
        "w_uq": nrm(ks[19], (N_B, Q_RANK, N_HEADS, QK_NOPE + QK_ROPE), Q_RANK),
        "w_out_b": nrm(ks[20], (N_B, N_HEADS * V_DIM, D_MODEL), N_HEADS * V_DIM),
        "final_norm": gain(ks[21], (D_MODEL,)),
    }


def reference(x, norm_a, w_in_a, conv_w, conv_b, w_rg, b_rg, w_ig, b_ig, lru_lambda, w_out_a,
              norm_kv, w_dkv, kv_norm, w_uk, w_uv,
              norm_b, w_in_b, q_norm, w_uq, w_out_b, final_norm):
    seq = x.shape[1]
    cos, sin = rope_tables(seq)
    k_nope = k_rope = v = None
    for layer in range(DEPTH):
        if layer < N_A:
            x = x + rglru_layer(x, norm_a[layer], w_in_a[layer], conv_w[layer], conv_b[layer],
                                w_rg[layer], b_rg[layer], w_ig[layer], b_ig[layer],
                                lru_lambda[layer], w_out_a[layer])
        else:
            if layer == N_A:
                k_nope, k_rope, v = mla_shared_kv(x, norm_kv, w_dkv, kv_norm, w_uk, w_uv, cos, sin)
            j = layer - N_A
            x = x + mla_layer(x, norm_b[j], w_in_b[j], q_norm[j], w_uq[j], w_out_b[j],
                              k_nope, k_rope, v, cos, sin)
    return rms_norm(x, final_norm)
```

```python
import numpy as np
import ml_dtypes
from contextlib import ExitStack
import concourse.bass as bass
import concourse.mybir as mybir
from concourse.bass_utils import run_bass_kernel_spmd

F32 = mybir.dt.float32
BF16 = mybir.dt.bfloat16
AF = mybir.ActivationFunctionType
ALU = mybir.AluOpType

D = 1024
DR = 1280
NCH = 10
H = 8
KVR = 256
QRK = 384
T = 256
NS = T // 128
EPS = 1e-6
ATTN_SCALE = 192.0 ** -0.5
N_CORES = 8

V_CW, V_CB, V_BRG, V_BIG, V_LAM, V_NA, V_NKV, V_NB, V_KVN, V_QN, NV = 0, 40, 50, 60, 70, 80, 88, 96, 104, 106, 109

G_LA = 4608
G_OUTA = 2560
G_DKV = 2560
G_UKV = 4096
G_CQ = 3072
G_GATE = 4096
G_UQ = 4608
G_OUTB = 4096
OFF_LA = 0
OFF_OUTA = OFF_LA + 5 * G_LA
OFF_DKV = OFF_OUTA + 4 * G_OUTA
OFF_UKV = OFF_DKV + G_DKV
OFF_CQ = OFF_UKV + G_UKV
OFF_GATE = OFF_CQ + G_CQ
OFF_UQ = OFF_GATE + 2 * G_GATE
OFF_OUTB = OFF_UQ + G_UQ
WTOT = OFF_OUTB + 2 * G_OUTB
SLOT = 4608
NSLOT = 3
PREF = NSLOT - 2

ENG = ("pe", "act", "dve", "pool", "sp")


import os
STOP = os.environ.get("KSTOP", "")


class _Stop(Exception):
    pass


_SC = [None]


def chk(name):
    if os.environ.get("KTRACE"):
        print("CHK", name, _SC[0].n)
    if STOP and name == STOP:
        raise _Stop()


class _Rec:
    def __init__(self):
        self.calls = []

    def __getattr__(self, name):
        def f(*a, **k):
            self.calls.append((name, a, k))
            return self
        return f


class Sched:
    def __init__(self, ndma=12):
        self.streams = {e: [] for e in ENG}
        self.cnt = {e: 0 for e in ENG}
        self.known = {e: {} for e in ENG}
        self.res = {}
        self.ndma = ndma
        self.dcnt = [0] * ndma
        self.drr = 0
        self.n = 0
        self.limit = int(os.environ.get("KMAXOPS", "100000000"))

    def _deps(self, eng, reads, writes):
        w = {}

        def need(tok, kind):
            if tok is None:
                return
            key, val = tok
            if key == eng:
                if eng == "pe" or kind != "raw":
                    return
            if w.get(key, 0) < val:
                w[key] = val

        for r in reads:
            st = self.res.get(r)
            if st:
                need(st[0], "raw")
        for x in writes:
            st = self.res.get(x)
            if st:
                need(st[0], "waw")
                for t in st[1].items():
                    need(t, "war")
        kn = self.known[eng]
        out = []
        for key, val in w.items():
            if kn.get(key, 0) < val:
                kn[key] = val
                out.append((key, val))
        return out

    def _commit(self, tok, reads, writes):
        for r in reads:
            st = self.res.setdefault(r, [None, {}])
            if st[1].get(tok[0], 0) < tok[1]:
                st[1][tok[0]] = tok[1]
        for x in writes:
            self.res[x] = [tok, {}]

    def op(self, eng, fn, reads=(), writes=()):
        self.n += 1
        if self.n > self.limit:
            return
        waits = self._deps(eng, reads, writes)
        self.cnt[eng] += 1
        tok = (eng, self.cnt[eng])
        self._commit(tok, reads, writes)
        self._emit(eng, waits, fn, tok, 1)

    def dma(self, fn, reads=(), writes=(), eng="sp"):
        self.n += 1
        if self.n > self.limit:
            return
        waits = self._deps(eng, reads, writes)
        k = self.drr
        self.drr = (k + 1) % self.ndma
        key = ("d", k)
        prev = self.dcnt[k] * 16
        if prev and self.known[eng].get(key, 0) < prev:
            self.known[eng][key] = prev
            waits.append((key, prev))
        self.dcnt[k] += 1
        tok = (key, self.dcnt[k] * 16)
        self._commit(tok, reads, writes)
        self._emit(eng, waits, fn, tok, 16)

    def _emit(self, eng, waits, fn, tok, inc):
        rec = _Rec()
        fn(rec)
        if os.environ.get("KOPTRACE"):
            lo, hi = (int(v) for v in os.environ["KOPTRACE"].split(","))
            if lo <= self.n <= hi:
                print("OP", self.n, eng, [c[0] for c in rec.calls], waits)
        self.streams[eng].append((waits, rec.calls, tok, inc))


def build(nseq, S):
    NT = S // T
    KT = S // 128
    NTOK = nseq * S
    nc = bass.Bass("TRN2", target_bir_lowering=False)

    def din(name, shape, dt=F32):
        return nc.dram_tensor(name, list(shape), dt, kind="ExternalInput").ap()

    x = din("x", [NTOK, D])
    out = nc.dram_tensor("out", [NTOK, D], F32, kind="ExternalOutput").ap()
    w_in_a = din("w_in_a", [D, 2 * DR])
    w_rg = din("w_rg", [NCH, 128, 128])
    w_ig = din("w_ig", [NCH, 128, 128])
    w_out_a = din("w_out_a", [DR, D])
    w_dkv = din("w_dkv", [D, 320])
    w_uk = din("w_uk", [KVR, 1024])
    w_uv = din("w_uv", [KVR, 1024])
    w_in_b = din("w_in_b", [D, 1408])
    w_uq = din("w_uq", [QRK, H, 192])
    w_out_b = din("w_out_b", [1024, 1024])
    vec_d = din("vec", [128, NV])
    gfin_d = din("gfin", [128, D])
    cs_d = din("cs", [S, 128])
    ident_d = din("ident", [128, 128])
    tri_d = din("tri", [128, 128])
    wbf = nc.dram_tensor("wbf", [128, WTOT], BF16, kind="Internal").ap()

    sc = Sched()
    _SC[0] = sc
    sems = {}
    es = ExitStack()
    with es:
        for e in ("pe", "act", "dve", "pool", "sp"):
            sems[e] = es.enter_context(nc.semaphore("s_" + e))
        for k in range(sc.ndma):
            sems[("d", k)] = es.enter_context(nc.semaphore("s_d%d" % k))

        sc.sems = sems
        sc.E = {}

        def run_block(emit, final=False):
            try:
                emit()
            except _Stop:
                pass
            with nc.Block() as block:
                def mk(name):
                    def body(e):
                        for waits, calls, tok, inc in sc.streams[name]:
                            for key, val in waits:
                                e.wait_ge(sems[key], val)
                            ins = None
                            for (mname, a, k) in calls:
                                ins = getattr(e, mname)(*a, **k)
                            ins.then_inc(sems[tok[0]], inc)
                        if final and name == "sp":
                            for k in range(sc.ndma):
                                if sc.dcnt[k]:
                                    e.wait_ge(sems[("d", k)], sc.dcnt[k] * 16)
                    return body
                block.sync(mk("sp"))
                block.tensor(mk("pe"))
                block.scalar(mk("act"))
                block.vector(mk("dve"))
                block.gpsimd(mk("pool"))

        def sb(stack, name, shape, dt):
            return stack.enter_context(nc.sbuf_tensor("sb_" + name, list(shape), dt))

        def emit_all():
            vec = sb(es, "vec", [128, NV], F32)
            der = sb(es, "der", [128, 48], F32)
            tmpv = sb(es, "tmpv", [128, 80], F32)
            identf = sb(es, "identf", [128, 128], F32)
            identb = sb(es, "identb", [128, 128], BF16)
            trif = sb(es, "trif", [128, 128], F32)
            trib = sb(es, "trib", [128, 128], BF16)
            halfc = sb(es, "halfc", [128, T], F32)
            mhalf = sb(es, "mhalf", [128, 4], F32)
            bsc = sb(es, "bsc", [128, 8], F32)
            epsc = sb(es, "epsc", [128, 2], F32)

            sc.dma(lambda e: e.dma_start(out=vec[:], in_=vec_d[:, :]), writes=["vec"])
            sc.dma(lambda e: e.dma_start(out=identf[:], in_=ident_d[:, :]), writes=["identf"])
            sc.dma(lambda e: e.dma_start(out=trif[:], in_=tri_d[:, :]), writes=["trif"])
            sc.op("pool", lambda e: e.memset(halfc[:], 0.5), writes=["halfc"])
            sc.op("pool", lambda e: e.memset(mhalf[:], -0.5), writes=["mhalf"])
            sc.op("pool", lambda e: e.memset(epsc[:, 0:1], EPS), writes=["epsc0"])
            sc.op("pool", lambda e: e.memset(epsc[:, 1:2], 0.25), writes=["epsc1"])
            sc.op("dve", lambda e: e.tensor_copy(out=identb[:], in_=identf[:]), reads=["identf"], writes=["identb"])
            sc.op("dve", lambda e: e.tensor_copy(out=trib[:], in_=trif[:]), reads=["trif"], writes=["trib"])
            sc.op("dve", lambda e: e.tensor_scalar(out=der[:, 0:20], in0=vec[:, V_BRG:V_BRG + 20], scalar1=0.5,
                                                    scalar2=None, op0=ALU.mult), reads=["vec"], writes=["der_hb"])
            lam = vec[:, V_LAM:V_LAM + 10]
            t_al, t_ex, t_z, t_z2, t_p, t_r = (tmpv[:, i * 10:(i + 1) * 10] for i in range(6))
            sc.op("act", lambda e: e.activation(out=t_al, in_=lam, func=AF.Abs),
                  reads=["vec"], writes=["t_al"])
            sc.op("act", lambda e: e.activation(out=t_ex, in_=t_al, func=AF.Exp, scale=-1.0), reads=["t_al"], writes=["t_ex"])
            sc.op("dve", lambda e: e.tensor_scalar(out=t_z, in0=t_ex, scalar1=2.0, scalar2=None, op0=ALU.add),
                  reads=["t_ex"], writes=["t_z"])
            sc.op("dve", lambda e: e.reciprocal(out=t_z, in_=t_z), reads=["t_z"], writes=["t_z"])
            sc.op("dve", lambda e: e.tensor_tensor(out=t_z, in0=t_z, in1=t_ex, op=ALU.mult), reads=["t_z", "t_ex"], writes=["t_z"])
            sc.op("dve", lambda e: e.tensor_tensor(out=t_z2, in0=t_z, in1=t_z, op=ALU.mult), reads=["t_z"], writes=["t_z2"])
            sc.op("dve", lambda e: e.tensor_scalar(out=t_p, in0=t_z2, scalar1=1.0 / 13, scalar2=1.0 / 11, op0=ALU.mult,
                                                    op1=ALU.add), reads=["t_z2"], writes=["t_p"])
            for cst in (1.0 / 9, 1.0 / 7, 1.0 / 5, 1.0 / 3, 1.0):
                sc.op("dve", lambda e: e.tensor_tensor(out=t_p, in0=t_p, in1=t_z2, op=ALU.mult), reads=["t_p", "t_z2"], writes=["t_p"])
                sc.op("dve", lambda e, cst=cst: e.tensor_scalar(out=t_p, in0=t_p, scalar1=cst, scalar2=None, op0=ALU.add),
                      reads=["t_p"], writes=["t_p"])
            sc.op("dve", lambda e: e.tensor_tensor(out=t_p, in0=t_p, in1=t_z, op=ALU.mult), reads=["t_p", "t_z"], writes=["t_p"])
            sc.op("dve", lambda e: e.tensor_scalar(out=t_r, in0=lam, scalar1=-1.0, scalar2=0.0, op0=ALU.mult, op1=ALU.max),
                  reads=["vec"], writes=["t_r"])
            sc.op("dve", lambda e: e.scalar_tensor_tensor(out=t_r, in0=t_p, scalar=2.0, in1=t_r, op0=ALU.mult, op1=ALU.add),
                  reads=["t_p", "t_r"], writes=["t_r"])
            sc.op("dve", lambda e: e.tensor_scalar(out=der[:, 20:30], in0=t_r, scalar1=-8.0, scalar2=None, op0=ALU.mult),
                  reads=["t_r"], writes=["der_sc"])
            sc.op("dve", lambda e: e.tensor_scalar(out=der[:, 30:40], in0=t_r, scalar1=-4.0, scalar2=None, op0=ALU.mult),
                  reads=["t_r"], writes=["der_sch"])

            chk("consts")
            ps_ = ExitStack()
            with ps_:
                imgLA = sb(ps_, "imgLA", [128, NCH, 2304], BF16)
                imgOA = sb(ps_, "imgOA", [128, 2, 2, 5, 512], BF16)
                imgDKV = sb(ps_, "imgDKV", [128, 8, 320], BF16)
                imgUKV = sb(ps_, "imgUKV", [128, 2, 2, 1024], BF16)
                imgCQ = sb(ps_, "imgCQ", [128, 8, 384], BF16)
                imgGT = sb(ps_, "imgGT", [128, 2, 8, 512], BF16)
                imgUQ = sb(ps_, "imgUQ", [128, 3, 1536], BF16)
                imgOB = sb(ps_, "imgOB", [128, 2, 8, 512], BF16)
                NSTG = 3
                stg = [sb(ps_, "stg%d" % i, [128, 2560], F32) for i in range(NSTG)]
                st_i = [0]
                ce_i = [0]
                cast_engs = ("dve", "act")

                def stage_load(src_ap, ncols):
                    i = st_i[0] % NSTG
                    st_i[0] += 1
                    sc.dma(lambda e: e.dma_start(out=stg[i][:, 0:ncols], in_=src_ap), writes=[("stg", i)])
                    return i

                def stage_load3(src_ap, a, b):
                    i = st_i[0] % NSTG
                    st_i[0] += 1
                    dst = stg[i][:, 0:a * b].rearrange("p (a b) -> p a b", b=b)
                    sc.dma(lambda e: e.dma_start(out=dst, in_=src_ap), writes=[("stg", i)])
                    return i

                def cast(dst, src, scale, reads, writes):
                    eng = cast_engs[ce_i[0] % len(cast_engs)]
                    ce_i[0] += 1
                    if eng == "act":
                        if scale is None:
                            sc.op("act", lambda e: e.activation(out=dst, in_=src, func=AF.Copy), reads=reads, writes=writes)
                        else:
                            sc.op("act", lambda e: e.activation(out=dst, in_=src, func=AF.Copy, scale=scale),
                                  reads=reads + ["vec"], writes=writes)
                    else:
                        if scale is None:
                            sc.op(eng, lambda e: e.tensor_copy(out=dst, in_=src), reads=reads, writes=writes)
                        else:
                            sc.op(eng, lambda e: e.tensor_scalar(out=dst, in0=src, scalar1=scale, scalar2=None, op0=ALU.mult),
                                  reads=reads + ["vec"], writes=writes)

                for kc in range(8):
                    i = stage_load(w_in_a[kc * 128:(kc + 1) * 128, :], 2560)
                    g = vec[:, V_NA + kc:V_NA + kc + 1]
                    cast(imgLA[:, :, kc * 128:(kc + 1) * 128], stg[i][:, 0:1280].rearrange("p (c n) -> p c n", n=128), g,
                         [("stg", i)], [("imgLA", kc, 0)])
                    cast(imgLA[:, :, 1024 + kc * 128:1024 + (kc + 1) * 128],
                         stg[i][:, 1280:2560].rearrange("p (c n) -> p c n", n=128), g, [("stg", i)], [("imgLA", kc, 1)])
                i = stage_load3(w_rg.rearrange("c i o -> i c o"), NCH, 128)
                cast(imgLA[:, :, 2048:2176], stg[i][:, 0:1280].rearrange("p (c n) -> p c n", n=128), None, [("stg", i)],
                     [("imgLA", 8, 0)])
                i = stage_load3(w_ig.rearrange("c i o -> i c o"), NCH, 128)
                cast(imgLA[:, :, 2176:2304], stg[i][:, 0:1280].rearrange("p (c n) -> p c n", n=128), None, [("stg", i)],
                     [("imgLA", 8, 1)])
                la_w = [("imgLA", kc, hh) for kc in range(9) for hh in range(2)]
                for g5 in range(5):
                    sc.dma(lambda e, g5=g5: e.dma_start(out=wbf[:, OFF_LA + g5 * G_LA:OFF_LA + (g5 + 1) * G_LA],
                                                        in_=imgLA[:, 2 * g5:2 * g5 + 2, :]), reads=la_w, writes=[("wbf", "LA", g5)])
                chk("pro_la")
                for kc in range(NCH):
                    i = stage_load(w_out_a[kc * 128:(kc + 1) * 128, :], 1024)
                    cast(imgOA[:, :, kc // 5, kc % 5, :], stg[i][:, 0:1024].rearrange("p (h n) -> p h n", n=512), None,
                         [("stg", i)], [("imgOA", kc)])
                for hf in range(2):
                    for kg in range(2):
                        o = OFF_OUTA + (hf * 2 + kg) * G_OUTA
                        sc.dma(lambda e, hf=hf, kg=kg, o=o: e.dma_start(out=wbf[:, o:o + G_OUTA], in_=imgOA[:, hf, kg, :, :]),
                               reads=[("imgOA", kc) for kc in range(NCH)], writes=[("wbf", "OA", hf, kg)])
                i = stage_load3(w_dkv.rearrange("(k p) n -> p k n", p=128), 8, 320)
                for kc in range(8):
                    cast(imgDKV[:, kc, :], stg[i][:, kc * 320:(kc + 1) * 320], vec[:, V_NKV + kc:V_NKV + kc + 1], [("stg", i)],
                         [("imgDKV", kc)])
                sc.dma(lambda e: e.dma_start(out=wbf[:, OFF_DKV:OFF_DKV + G_DKV], in_=imgDKV[:]),
                       reads=[("imgDKV", kc) for kc in range(8)], writes=[("wbf", "DKV")])
                for wi, wsrc in enumerate((w_uk, w_uv)):
                    i = stage_load3(wsrc.rearrange("(k p) n -> p k n", p=128), 2, 1024)
                    for kc in range(2):
                        cast(imgUKV[:, wi, kc, :], stg[i][:, kc * 1024:(kc + 1) * 1024], vec[:, V_KVN + kc:V_KVN + kc + 1],
                             [("stg", i)], [("imgUKV", wi, kc)])
                sc.dma(lambda e: e.dma_start(out=wbf[:, OFF_UKV:OFF_UKV + G_UKV], in_=imgUKV[:]),
                       reads=[("imgUKV", a, b) for a in range(2) for b in range(2)], writes=[("wbf", "UKV")])
                for kc in range(8):
                    i = stage_load(w_in_b[kc * 128:(kc + 1) * 128, :], 1408)
                    g = vec[:, V_NB + kc:V_NB + kc + 1]
                    cast(imgCQ[:, kc, :], stg[i][:, 0:384], g, [("stg", i)], [("imgCQ", kc)])
                    cast(imgGT[:, :, kc, :], stg[i][:, 384:1408].rearrange("p (h n) -> p h n", n=512), g, [("stg", i)],
                         [("imgGT", kc)])
                sc.dma(lambda e: e.dma_start(out=wbf[:, OFF_CQ:OFF_CQ + G_CQ], in_=imgCQ[:]),
                       reads=[("imgCQ", kc) for kc in range(8)], writes=[("wbf", "CQ")])
                for gh in range(2):
                    o = OFF_GATE + gh * G_GATE
                    sc.dma(lambda e, gh=gh, o=o: e.dma_start(out=wbf[:, o:o + G_GATE], in_=imgGT[:, gh, :, :]),
                           reads=[("imgGT", kc) for kc in range(8)], writes=[("wbf", "GT", gh)])
                for kc in range(3):
                    i = stage_load3(w_uq[kc * 128:(kc + 1) * 128, :, :], H, 192)
                    g = vec[:, V_QN + kc:V_QN + kc + 1]
                    s3 = stg[i][:, 0:H * 192].rearrange("p (h n) -> p h n", n=192)
                    cast(imgUQ[:, kc, 0:1024].rearrange("p (h n) -> p h n", n=128), s3[:, :, 0:128], g, [("stg", i)],
                         [("imgUQ", kc, 0)])
                    cast(imgUQ[:, kc, 1024:1536].rearrange("p (h n) -> p h n", n=64), s3[:, :, 128:192], g, [("stg", i)],
                         [("imgUQ", kc, 1)])
                sc.dma(lambda e: e.dma_start(out=wbf[:, OFF_UQ:OFF_UQ + G_UQ], in_=imgUQ[:]),
                       reads=[("imgUQ", a, b) for a in range(3) for b in range(2)], writes=[("wbf", "UQ")])
                for kc in range(8):
                    i = stage_load(w_out_b[kc * 128:(kc + 1) * 128, :], 1024)
                    cast(imgOB[:, :, kc, :], stg[i][:, 0:1024].rearrange("p (h n) -> p h n", n=512), None, [("stg", i)],
                         [("imgOB", kc)])
                for hf in range(2):
                    o = OFF_OUTB + hf * G_OUTB
                    sc.dma(lambda e, hf=hf, o=o: e.dma_start(out=wbf[:, o:o + G_OUTB], in_=imgOB[:, hf, :, :]),
                           reads=[("imgOB", kc) for kc in range(8)], writes=[("wbf", "OB", hf)])
                allw = [k for k in sc.res if isinstance(k, tuple) and k[0] == "wbf"]
                sc.op("act", lambda e: e.activation(out=bsc[:, 0:1], in_=vec[:, 0:1], func=AF.Copy), reads=allw, writes=["bsc0"])
                sc.op("dve", lambda e: e.memset(bsc[:, 1:2], 0.0), reads=allw, writes=["bsc1"])
                sc.dma(lambda e: e.dma_start(out=bsc[0:1, 4:8], in_=vec_d[0:1, 0:4]), reads=allw, writes=["bsc3"])

            chk("prologue")
            gfin = sb(es, "gfin", [128, D], F32)
            cst = sb(es, "cst", [128, KT, 128], F32)
            XT = [sb(es, "xt%d" % i, [128, NS, D], F32) for i in range(2)]
            hbf = [sb(es, "hbf%d" % i, [128, D], BF16) for i in range(NS)]
            hT = sb(es, "hT", [128, 8, T], BF16)
            xbuf = sb(es, "xbuf", [128, NCH, 4 + T], BF16)
            ROT = 2
            dk = [sb(es, "dk%d" % i, [128, 4, 128], BF16) for i in range(ROT)]
            xc = [sb(es, "xc%d" % i, [128, T], F32) for i in range(ROT)]
            xcb = [sb(es, "xcb%d" % i, [128, T], BF16) for i in range(ROT)]
            thr = [sb(es, "thr%d" % i, [128, T], F32) for i in range(ROT)]
            thi = [sb(es, "thi%d" % i, [128, T], F32) for i in range(ROT)]
            a2b = [sb(es, "a2b%d" % i, [128, T], F32) for i in range(ROT)]
            hsb = [sb(es, "hsb%d" % i, [128, T], F32) for i in range(ROT)]
            thg = [sb(es, "thg%d" % i, [128, T], F32) for i in range(ROT)]
            hst = sb(es, "hst", [128, NCH], F32)
            yT = sb(es, "yT", [128, NCH, T], BF16)
            stat = sb(es, "stat", [128, 16], F32)
            obuf = sb(es, "obuf", [128, D], F32)
            junk = sb(es, "junk", [128, 384], BF16)
            ckv = [sb(es, "ckv%d" % i, [128, KVR], BF16) for i in range(NS)]
            kr2 = [sb(es, "kr2%d" % i, [128, 128], BF16) for i in range(NS)]
            rA = sb(es, "rA", [128, 512], F32)
            rB = sb(es, "rB", [128, 512], F32)
            ckvT = sb(es, "ckvT", [128, 2, T], BF16)
            cq = [sb(es, "cq%d" % i, [128, QRK], BF16) for i in range(NS)]
            cqT = sb(es, "cqT", [128, 3, T], BF16)
            sg = sb(es, "sg", [128, 8, T], F32)
            qn = sb(es, "qn", [128, H, T], BF16)
            qrtok = [sb(es, "qrtok%d" % i, [128, 512], BF16) for i in range(NS)]
            qrT = sb(es, "qrT", [128, 4, T], BF16)
            NPT = 4
            pT = [sb(es, "pT%d" % i, [128, T], BF16) for i in range(NPT)]
            obf = sb(es, "obf", [128, NS, D], BF16)
            y2T = sb(es, "y2T", [128, 8, T], BF16)
            kTn = sb(es, "kTn", [128, H, S], BF16)
            kTr = sb(es, "kTr", [128, S], BF16)
            Vaug = sb(es, "Vaug", [128, KT, H, 129], BF16)
            slots = [sb(es, "slot%d" % i, [128, SLOT], BF16) for i in range(NSLOT)]
            banks = [es.enter_context(nc.psum_tensor("bank%d" % i, [128, 512], F32)) for i in range(8)]

            pools = {"g": [0, 1, 2], "s": [3, 4], "o": [5, 6], "t": [7]}
            prr = {k: 0 for k in pools}

            def bank(role):
                lst = pools[role]
                i = lst[prr[role] % len(lst)]
                prr[role] += 1
                return i

            def B(i):
                return ("bank", i)

            sc.dma(lambda e: e.dma_start(out=gfin[:], in_=gfin_d[:, :]), writes=["gfin"])
            sc.dma(lambda e: e.dma_start(out=cst[:], in_=cs_d.rearrange("(k p) c -> p k c", p=128)), writes=["cst"])
            sc.op("dve", lambda e: e.memset(Vaug[:, :, :, 128:129], 1.0), writes=["vones"])

            uses = []
            for ti in range(nseq * NT):
                for g5 in range(5):
                    uses.append((OFF_LA + g5 * G_LA, G_LA, ("wbf", "LA", g5)))
                for hf in range(2):
                    for kg in range(2):
                        uses.append((OFF_OUTA + (hf * 2 + kg) * G_OUTA, G_OUTA, ("wbf", "OA", hf, kg)))
                uses.append((OFF_DKV, G_DKV, ("wbf", "DKV")))
                uses.append((OFF_UKV, G_UKV, ("wbf", "UKV")))
                uses.append((OFF_CQ, G_CQ, ("wbf", "CQ")))
                for gh in range(2):
                    uses.append((OFF_GATE + gh * G_GATE, G_GATE, ("wbf", "GT", gh)))
                uses.append((OFF_UQ, G_UQ, ("wbf", "UQ")))
                for hf in range(2):
                    uses.append((OFF_OUTB + hf * G_OUTB, G_OUTB, ("wbf", "OB", hf)))
            UPT = 17
            loaded = [0]
            ucur = [0]

            def acquire():
                u = ucur[0]
                ucur[0] += 1
                while loaded[0] <= min(u + PREF, len(uses) - 1):
                    v = loaded[0]
                    off, n, rname = uses[v]
                    sl = v % NSLOT
                    sc.dma(lambda e, off=off, n=n, sl=sl: e.dma_start(out=slots[sl][:, 0:n], in_=wbf[:, off:off + n]),
                           reads=[rname], writes=[("slot", sl)])
                    loaded[0] += 1
                return u % NSLOT

            def load_x(ti):
                b, j = divmod(ti, NT)
                r0 = b * S + j * T
                buf = ti % 2
                sc.dma(lambda e: e.dma_start(out=XT[buf][:], in_=x[r0:r0 + T, :].rearrange("(s p) d -> p s d", p=128)),
                       writes=[("xt", buf, s, hf) for s in range(NS) for hf in range(2)])

            def rms_stats(xt, buf, col, n_feat, src_aps, src_reads, junk_aps, junk_res):
                for s in range(NS):
                    sc.op("act", lambda e, s=s: e.activation(out=junk_aps[s], in_=src_aps[s], func=AF.Square,
                                                              accum_out=stat[:, col + s:col + s + 1]),
                          reads=src_reads[s], writes=[junk_res[s], ("stat", col + s)])
                sc.op("act", lambda e: e.activation(out=stat[:, col:col + NS], in_=stat[:, col:col + NS], func=AF.Sqrt,
                                                     scale=1.0 / n_feat, bias=epsc[:, 0:1]),
                      reads=[("stat", col + s) for s in range(NS)] + ["epsc0"], writes=[("stat", col + s) for s in range(NS)])
                sc.op("dve", lambda e: e.reciprocal(out=stat[:, col:col + NS], in_=stat[:, col:col + NS]),
                      reads=[("stat", col + s) for s in range(NS)], writes=[("stat", col + s) for s in range(NS)])

            def norm_and_transpose(xt, buf, col):
                xr = [[("xt", buf, s, 0), ("xt", buf, s, 1)] for s in range(NS)]
                rms_stats(xt, buf, col, D, [xt[:, s, :] for s in range(NS)], xr, [hbf[s][:] for s in range(NS)],
                          [("hbf", s) for s in range(NS)])
                for s in range(NS):
                    sc.op("dve", lambda e, s=s: e.tensor_scalar(out=hbf[s][:], in0=xt[:, s, :], scalar1=stat[:, col + s:col + s + 1],
                                                                 scalar2=None, op0=ALU.mult),
                          reads=xr[s] + [("stat", col + s)], writes=[("hbf", s)])
                for g in range(2):
                    bi = bank("t") if g == 0 else bank("g")
                    bv = banks[bi][:].bitcast(BF16).rearrange("p (a t) -> p a t", t=T)

                    def tr(e, g=g, bv=bv):
                        ins = None
                        for k4 in range(4):
                            kc = g * 4 + k4
                            for s in range(NS):
                                ins = e.transpose(out=bv[:, k4, s * 128:(s + 1) * 128], in_=hbf[s][:, kc * 128:(kc + 1) * 128],
                                                  identity=identb[:])
                        return ins
                    sc.op("pe", tr, reads=[("hbf", s) for s in range(NS)], writes=[B(bi)])
                    eng = "act" if g == 0 else "dve"
                    if eng == "act":
                        sc.op("act", lambda e, g=g, bv=bv: e.activation(out=hT[:, g * 4:(g + 1) * 4, :], in_=bv, func=AF.Copy),
                              reads=[B(bi)], writes=[("hT", g)])
                    else:
                        sc.op("dve", lambda e, g=g, bv=bv: e.tensor_copy(out=hT[:, g * 4:(g + 1) * 4, :], in_=bv),
                              reads=[B(bi)], writes=[("hT", g)])

            HT = [("hT", 0), ("hT", 1)]

            NTILES = nseq * NT
            load_x(0)
            for ti in range(NTILES):
                b, j = divmod(ti, NT)
                r0 = b * S + j * T
                buf = ti % 2
                xt = XT[buf]
                if ti + 1 < NTILES:
                    load_x(ti + 1)
                if j == 0:
                    sc.op("dve", lambda e: e.memset(xbuf[:, :, 0:4], 0.0), writes=[("xbh", c) for c in range(NCH)])

                chk("t%d_start" % ti)
                norm_and_transpose(xt, buf, 0)
                chk("t%d_norm" % ti)
                st_slot = {}

                def stepA(c):
                    sl = acquire() if c % 2 == 0 else st_slot[c - 1]
                    st_slot[c] = sl
                    off = (c % 2) * 2304
                    bi = bank("g")

                    def mm(e):
                        ins = None
                        for kc in range(8):
                            ins = e.matmul(out=banks[bi][:, 0:T], lhsT=slots[sl][:, off + kc * 128:off + (kc + 1) * 128],
                                           rhs=hT[:, kc, :], start=(kc == 0), stop=(kc == 7))
                        return ins
                    sc.op("pe", mm, reads=[("slot", sl)] + HT, writes=[B(bi)])
                    sc.op("act", lambda e: e.activation(out=xbuf[:, c, 4:4 + T], in_=banks[bi][:, 0:T], func=AF.Copy),
                          reads=[B(bi)], writes=[("xbm", c)])
                    r = c % ROT
                    for k in range(4):
                        sc.op("act", lambda e, k=k: e.activation(out=dk[r][:, k, :], in_=identf[:], func=AF.Copy,
                                                                  scale=vec[:, V_CW + k * 10 + c:V_CW + k * 10 + c + 1]),
                              reads=[], writes=[("dk", r, k)])

                def stepB(c):
                    sl = st_slot[c]
                    off = (c % 2) * 2304
                    r = c % ROT
                    bi = bank("g")

                    def conv(e):
                        ins = None
                        for k in range(4):
                            ins = e.matmul(out=banks[bi][:, 0:T], lhsT=dk[r][:, k, :], rhs=xbuf[:, c, 1 + k:1 + k + T],
                                           start=(k == 0), stop=(k == 3))
                        return ins
                    sc.op("pe", conv, reads=[("xbm", c), ("xbh", c)] + [("dk", r, k) for k in range(4)], writes=[B(bi)])
                    sc.op("act", lambda e: e.activation(out=xc[r][:], in_=banks[bi][:, 0:T], func=AF.Identity,
                                                         bias=vec[:, V_CB + c:V_CB + c + 1]),
                          reads=[B(bi)], writes=[("xc", r)])
                    sc.op("dve", lambda e: e.tensor_copy(out=xcb[r][:], in_=xc[r][:]), reads=[("xc", r)], writes=[("xcb", r)])
                    bg = bank("g")

                    def mmg(e):
                        ins = None
                        for kc in range(8):
                            ins = e.matmul(out=banks[bg][:, 0:T],
                                           lhsT=slots[sl][:, off + 1024 + kc * 128:off + 1024 + (kc + 1) * 128],
                                           rhs=hT[:, kc, :], start=(kc == 0), stop=(kc == 7))
                        return ins
                    sc.op("pe", mmg, reads=[("slot", sl)] + HT, writes=[B(bg)])
                    sc.op("act", lambda e: e.activation(out=thg[r][:], in_=banks[bg][:, 0:T], func=AF.Tanh, scale=0.5),
                          reads=[B(bg)], writes=[("thg", r)])
                    sc.op("dve", lambda e: e.scalar_tensor_tensor(out=thg[r][:], in0=thg[r][:], scalar=1.0, in1=banks[bg][:, 0:T],
                                                                   op0=ALU.add, op1=ALU.mult),
                          reads=[B(bg), ("thg", r)], writes=[("thg", r)])

                def stepC(c):
                    sl = st_slot[c]
                    off = (c % 2) * 2304
                    r = c % ROT
                    bi = bank("g")

                    def ri(e):
                        e.matmul(out=banks[bi][:, 0:T], lhsT=slots[sl][:, off + 2048:off + 2176], rhs=xcb[r][:], start=True, stop=True)
                        return e.matmul(out=banks[bi][:, T:2 * T], lhsT=slots[sl][:, off + 2176:off + 2304], rhs=xcb[r][:],
                                        start=True, stop=True)
                    sc.op("pe", ri, reads=[("slot", sl), ("xcb", r)], writes=[B(bi)])
                    sc.op("act", lambda e: e.activation(out=thr[r][:], in_=banks[bi][:, 0:T], func=AF.Tanh, scale=0.5,
                                                         bias=der[:, c:c + 1]), reads=[B(bi)], writes=[("thr", r)])
                    sc.op("act", lambda e: e.activation(out=thi[r][:], in_=banks[bi][:, T:2 * T], func=AF.Tanh, scale=0.5,
                                                         bias=der[:, 10 + c:11 + c]), reads=[B(bi)], writes=[("thi", r)])
                    sc.op("act", lambda e: e.activation(out=a2b[r][:], in_=thr[r][:], func=AF.Exp, scale=der[:, 20 + c:21 + c],
                                                         bias=der[:, 20 + c:21 + c]), reads=[("thr", r)], writes=[("a2", r)])
                    sc.op("act", lambda e: e.activation(out=thr[r][:], in_=thr[r][:], func=AF.Exp, scale=der[:, 30 + c:31 + c],
                                                         bias=der[:, 30 + c:31 + c]), reads=[("thr", r)], writes=[("thr", r)])
                    sc.op("dve", lambda e: e.scalar_tensor_tensor(out=thi[r][:], in0=thi[r][:], scalar=1.0, in1=xc[r][:],
                                                                   op0=ALU.add, op1=ALU.mult),
                          reads=[("thi", r), ("xc", r)], writes=[("thi", r)])
                    sc.op("act", lambda e: e.activation(out=a2b[r][:], in_=a2b[r][:], func=AF.Sqrt, scale=-0.25,
                                                         bias=epsc[:, 1:2]), reads=[("a2", r), "epsc1"], writes=[("a2", r)])
                    sc.op("dve", lambda e: e.tensor_tensor(out=thi[r][:], in0=thi[r][:], in1=a2b[r][:], op=ALU.mult),
                          reads=[("thi", r), ("a2", r)], writes=[("thi", r)])
                    init = 0.0 if j == 0 else hst[:, c:c + 1]
                    sc.op("dve", lambda e: e.tensor_tensor_scan(out=hsb[r][:], data0=thr[r][:], data1=thi[r][:], initial=init,
                                                                 op0=ALU.mult, op1=ALU.add),
                          reads=[("thr", r), ("thi", r), ("hst", c)], writes=[("hs", r)])
                    sc.op("dve", lambda e: e.tensor_copy(out=hst[:, c:c + 1], in_=hsb[r][:, T - 1:T]), reads=[("hs", r)],
                          writes=[("hst", c)])
                    sc.op("dve", lambda e: e.scalar_tensor_tensor(out=yT[:, c, :], in0=hsb[r][:], scalar=0.5, in1=thg[r][:],
                                                                   op0=ALU.mult, op1=ALU.mult),
                          reads=[("hs", r), ("thg", r)], writes=[("yT", c)])

                for step in range(NCH + 2):
                    if step < NCH:
                        stepA(step)
                    if 0 <= step - 1 < NCH:
                        stepB(step - 1)
                    if 0 <= step - 2 < NCH:
                        stepC(step - 2)
                sc.op("dve", lambda e: e.tensor_copy(out=xbuf[:, :, 0:4], in_=xbuf[:, :, T:T + 4]),
                      reads=[("xbm", c) for c in range(NCH)], writes=[("xbh", c) for c in range(NCH)])

                chk("t%d_A" % ti)
                for hf in range(2):
                    bo = [bank("o") for _ in range(NS)]
                    for kg in range(2):
                        sl = acquire()
                        for s in range(NS):
                            def mo(e, s=s, kg=kg, sl=sl):
                                ins = None
                                for k5 in range(5):
                                    kc = kg * 5 + k5
                                    ins = e.matmul(out=banks[bo[s]][:, 0:512], lhsT=yT[:, kc, s * 128:(s + 1) * 128],
                                                   rhs=slots[sl][:, k5 * 512:(k5 + 1) * 512], start=(kc == 0), stop=(kc == NCH - 1))
                                return ins
                            sc.op("pe", mo, reads=[("slot", sl)] + [("yT", kg * 5 + k5) for k5 in range(5)], writes=[B(bo[s])])
                    for s in range(NS):
                        sc.op("dve", lambda e, s=s, hf=hf: e.tensor_tensor(out=xt[:, s, hf * 512:(hf + 1) * 512],
                                                                            in0=xt[:, s, hf * 512:(hf + 1) * 512],
                                                                            in1=banks[bo[s]][:, 0:512], op=ALU.add),
                              reads=[B(bo[s]), ("xt", buf, s, hf)], writes=[("xt", buf, s, hf)])

                chk("t%d_Aout" % ti)
                norm_and_transpose(xt, buf, 2)
                sl = acquire()
                bck = []
                for s in range(NS):
                    bi = bank("g")
                    bck.append(bi)

                    def mck(e, s=s, bi=bi, sl=sl):
                        ins = None
                        for kc in range(8):
                            ins = e.matmul(out=banks[bi][:, 0:320], lhsT=hT[:, kc, s * 128:(s + 1) * 128],
                                           rhs=slots[sl][:, kc * 320:(kc + 1) * 320], start=(kc == 0), stop=(kc == 7))
                        return ins
                    sc.op("pe", mck, reads=[("slot", sl)] + HT, writes=[B(bi)])
                rms_stats(xt, buf, 4, KVR, [banks[bck[s]][:, 0:KVR] for s in range(NS)], [[B(bck[s])] for s in range(NS)],
                          [junk[:, 0:KVR] for s in range(NS)], ["junk"] * NS)
                for s in range(NS):
                    kt = j * NS + s
                    bi = bck[s]
                    sc.op("dve", lambda e, s=s, bi=bi: e.tensor_scalar(out=ckv[s][:], in0=banks[bi][:, 0:KVR],
                                                                        scalar1=stat[:, 4 + s:5 + s], scalar2=None, op0=ALU.mult),
                          reads=[B(bi), ("stat", 4 + s)], writes=[("ckv", s)])
                    sc.op("dve", lambda e, bi=bi, kt=kt: e.tensor_tensor(out=rA[:, 0:64], in0=banks[bi][:, 256:320],
                                                                          in1=cst[:, kt, 0:64], op=ALU.mult),
                          reads=[B(bi), "cst"], writes=["rA"])
                    sc.op("dve", lambda e, bi=bi, kt=kt: e.tensor_tensor(out=rB[:, 0:32], in0=banks[bi][:, 288:320],
                                                                          in1=cst[:, kt, 64:96], op=ALU.mult),
                          reads=[B(bi), "cst"], writes=["rB0"])
                    sc.op("dve", lambda e, bi=bi, kt=kt: e.tensor_tensor(out=rB[:, 32:64], in0=banks[bi][:, 256:288],
                                                                          in1=cst[:, kt, 96:128], op=ALU.mult),
                          reads=[B(bi), "cst"], writes=["rB1"])
                    for dup in range(2):
                        sc.op("dve", lambda e, s=s, dup=dup: e.tensor_tensor(out=kr2[s][:, dup * 64:(dup + 1) * 64], in0=rA[:, 0:64],
                                                                              in1=rB[:, 0:64], op=ALU.add),
                              reads=["rA", "rB0", "rB1"], writes=[("kr2", s, dup)])
                bi = bank("t")
                bv = banks[bi][:].bitcast(BF16)

                def trk(e, bv=bv):
                    ins = None
                    for s in range(NS):
                        for kc in range(2):
                            ins = e.transpose(out=bv[:, kc * T + s * 128:kc * T + (s + 1) * 128], in_=ckv[s][:, kc * 128:(kc + 1) * 128],
                                              identity=identb[:])
                        ins = e.transpose(out=bv[:, 2 * T + s * 128:2 * T + (s + 1) * 128], in_=kr2[s][:], identity=identb[:])
                    return ins
                sc.op("pe", trk, reads=[("ckv", s) for s in range(NS)] + [("kr2", s, d) for s in range(NS) for d in range(2)],
                      writes=[B(bi)])
                sc.op("act", lambda e, bv=bv: e.activation(out=ckvT[:], in_=bv[:, 0:2 * T].rearrange("p (a t) -> p a t", t=T),
                                                            func=AF.Copy), reads=[B(bi)], writes=["ckvT"])
                sc.op("act", lambda e, bv=bv: e.activation(out=kTr[:, j * T:(j + 1) * T], in_=bv[:, 2 * T:3 * T], func=AF.Copy),
                      reads=[B(bi)], writes=[("kvr", j)])
                sl = acquire()
                for hp in range(4):
                    bi = bank("g")

                    def mkn(e, hp=hp, bi=bi, sl=sl):
                        ins = None
                        for hh in range(2):
                            h = 2 * hp + hh
                            for kc in range(2):
                                ins = e.matmul(out=banks[bi][:, hh * T:(hh + 1) * T],
                                               lhsT=slots[sl][:, kc * 1024 + h * 128:kc * 1024 + (h + 1) * 128],
                                               rhs=ckvT[:, kc, :], start=(kc == 0), stop=(kc == 1))
                        return ins
                    sc.op("pe", mkn, reads=[("slot", sl), "ckvT"], writes=[B(bi)])
                    src = banks[bi][:, 0:2 * T].rearrange("p (a t) -> p a t", t=T)
                    dst = kTn[:, 2 * hp:2 * hp + 2, j * T:(j + 1) * T]
                    if hp % 2 == 0:
                        sc.op("act", lambda e, src=src, dst=dst: e.activation(out=dst, in_=src, func=AF.Copy), reads=[B(bi)],
                              writes=[("kvn", j, hp)])
                    else:
                        sc.op("dve", lambda e, src=src, dst=dst: e.tensor_copy(out=dst, in_=src), reads=[B(bi)],
                              writes=[("kvn", j, hp)])
                for s in range(NS):
                    kt = j * NS + s
                    for hf in range(2):
                        bi = bank("g")

                        def mv(e, s=s, hf=hf, bi=bi, sl=sl):
                            ins = None
                            for kc in range(2):
                                ins = e.matmul(out=banks[bi][:, 0:512], lhsT=ckvT[:, kc, s * 128:(s + 1) * 128],
                                               rhs=slots[sl][:, 2048 + kc * 1024 + hf * 512:2048 + kc * 1024 + (hf + 1) * 512],
                                               start=(kc == 0), stop=(kc == 1))
                            return ins
                        sc.op("pe", mv, reads=[("slot", sl), "ckvT"], writes=[B(bi)])
                        src = banks[bi][:, 0:512].rearrange("p (a t) -> p a t", t=128)
                        dst = Vaug[:, kt, 4 * hf:4 * hf + 4, 0:128]
                        if hf == 0:
                            sc.op("act", lambda e, src=src, dst=dst: e.activation(out=dst, in_=src, func=AF.Copy), reads=[B(bi)],
                                  writes=[("kvv", kt, hf)])
                        else:
                            sc.op("dve", lambda e, src=src, dst=dst: e.tensor_copy(out=dst, in_=src), reads=[B(bi)],
                                  writes=[("kvv", kt, hf)])

                chk("t%d_KV" % ti)
                sl = acquire()
                bcq = []
                for s in range(NS):
                    bi = bank("g")
                    bcq.append(bi)

                    def mcq(e, s=s, bi=bi, sl=sl):
                        ins = None
                        for kc in range(8):
                            ins = e.matmul(out=banks[bi][:, 0:QRK], lhsT=hT[:, kc, s * 128:(s + 1) * 128],
                                           rhs=slots[sl][:, kc * QRK:(kc + 1) * QRK], start=(kc == 0), stop=(kc == 7))
                        return ins
                    sc.op("pe", mcq, reads=[("slot", sl)] + HT, writes=[B(bi)])
                rms_stats(xt, buf, 6, QRK, [banks[bcq[s]][:, 0:QRK] for s in range(NS)], [[B(bcq[s])] for s in range(NS)],
                          [junk[:, 0:QRK] for s in range(NS)], ["junk"] * NS)
                for s in range(NS):
                    bi = bcq[s]
                    sc.op("dve", lambda e, s=s, bi=bi: e.tensor_scalar(out=cq[s][:], in0=banks[bi][:, 0:QRK],
                                                                        scalar1=stat[:, 6 + s:7 + s], scalar2=None, op0=ALU.mult),
                          reads=[B(bi), ("stat", 6 + s)], writes=[("cq", s)])
                bi = bank("t")
                bv = banks[bi][:].bitcast(BF16)

                def trq(e, bv=bv):
                    ins = None
                    for s in range(NS):
                        for kc in range(3):
                            ins = e.transpose(out=bv[:, kc * T + s * 128:kc * T + (s + 1) * 128], in_=cq[s][:, kc * 128:(kc + 1) * 128],
                                              identity=identb[:])
                    return ins
                sc.op("pe", trq, reads=[("cq", s) for s in range(NS)], writes=[B(bi)])
                sc.op("act", lambda e, bv=bv: e.activation(out=cqT[:], in_=bv[:, 0:3 * T].rearrange("p (a t) -> p a t", t=T),
                                                            func=AF.Copy), reads=[B(bi)], writes=["cqT"])
                for gh in range(2):
                    sl = acquire()
                    for h4 in range(4):
                        hc = gh * 4 + h4
                        bi = bank("g")

                        def mg(e, h4=h4, bi=bi, sl=sl):
                            ins = None
                            for kc in range(8):
                                ins = e.matmul(out=banks[bi][:, 0:T], lhsT=slots[sl][:, kc * 512 + h4 * 128:kc * 512 + (h4 + 1) * 128],
                                               rhs=hT[:, kc, :], start=(kc == 0), stop=(kc == 7))
                            return ins
                        sc.op("pe", mg, reads=[("slot", sl)] + HT, writes=[B(bi)])
                        sc.op("act", lambda e, hc=hc, bi=bi: e.activation(out=sg[:, hc, :], in_=banks[bi][:, 0:T], func=AF.Tanh,
                                                                           scale=0.5), reads=[B(bi)], writes=[("sg", hc)])
                        sc.op("dve", lambda e, hc=hc, bi=bi: e.scalar_tensor_tensor(out=sg[:, hc, :], in0=sg[:, hc, :], scalar=1.0,
                                                                                     in1=banks[bi][:, 0:T], op0=ALU.add, op1=ALU.mult),
                              reads=[B(bi), ("sg", hc)], writes=[("sg", hc)])
                sl = acquire()
                for hp in range(4):
                    bi = bank("g")

                    def mqn(e, hp=hp, bi=bi, sl=sl):
                        ins = None
                        for hh in range(2):
                            h = 2 * hp + hh
                            for kc in range(3):
                                ins = e.matmul(out=banks[bi][:, hh * T:(hh + 1) * T],
                                               lhsT=slots[sl][:, kc * 1536 + h * 128:kc * 1536 + (h + 1) * 128],
                                               rhs=cqT[:, kc, :], start=(kc == 0), stop=(kc == 2))
                        return ins
                    sc.op("pe", mqn, reads=[("slot", sl), "cqT"], writes=[B(bi)])
                    src = banks[bi][:, 0:2 * T].rearrange("p (a t) -> p a t", t=T)
                    dst = qn[:, 2 * hp:2 * hp + 2, :]
                    if hp % 2 == 0:
                        sc.op("act", lambda e, src=src, dst=dst: e.activation(out=dst, in_=src, func=AF.Copy), reads=[B(bi)],
                              writes=[("qn", hp)])
                    else:
                        sc.op("dve", lambda e, src=src, dst=dst: e.tensor_copy(out=dst, in_=src), reads=[B(bi)], writes=[("qn", hp)])
                for s in range(NS):
                    kt = j * NS + s
                    bi = bank("g")

                    def mqr(e, s=s, bi=bi, sl=sl):
                        ins = None
                        for kc in range(3):
                            ins = e.matmul(out=banks[bi][:, 0:512], lhsT=cqT[:, kc, s * 128:(s + 1) * 128],
                                           rhs=slots[sl][:, kc * 1536 + 1024:kc * 1536 + 1536], start=(kc == 0), stop=(kc == 2))
                        return ins
                    sc.op("pe", mqr, reads=[("slot", sl), "cqT"], writes=[B(bi)])
                    X = banks[bi][:, 0:512].rearrange("p (h r) -> p h r", r=64)
                    cc = cst[:, kt, 0:64].unsqueeze(1).broadcast_to([128, H, 64])
                    ss0 = cst[:, kt, 64:96].unsqueeze(1).broadcast_to([128, H, 32])
                    ss1 = cst[:, kt, 96:128].unsqueeze(1).broadcast_to([128, H, 32])
                    A3 = rA[:, 0:512].rearrange("p (h r) -> p h r", r=64)
                    B3 = rB[:, 0:512].rearrange("p (h r) -> p h r", r=64)
                    sc.op("dve", lambda e, X=X, cc=cc, A3=A3: e.tensor_tensor(out=A3, in0=X, in1=cc, op=ALU.mult), reads=[B(bi), "cst"],
                          writes=["rA"])
                    sc.op("dve", lambda e, X=X, ss0=ss0, B3=B3: e.tensor_tensor(out=B3[:, :, 0:32], in0=X[:, :, 32:64], in1=ss0,
                                                                                 op=ALU.mult), reads=[B(bi), "cst"], writes=["rB0"])
                    sc.op("dve", lambda e, X=X, ss1=ss1, B3=B3: e.tensor_tensor(out=B3[:, :, 32:64], in0=X[:, :, 0:32], in1=ss1,
                                                                                 op=ALU.mult), reads=[B(bi), "cst"], writes=["rB1"])
                    sc.op("dve", lambda e, s=s: e.tensor_tensor(out=qrtok[s][:], in0=rA[:, 0:512], in1=rB[:, 0:512], op=ALU.add),
                          reads=["rA", "rB0", "rB1"], writes=[("qrtok", s)])
                bi = bank("t")
                bv = banks[bi][:].bitcast(BF16).rearrange("p (a t) -> p a t", t=T)

                def trr(e, bv=bv):
                    ins = None
                    for hp in range(4):
                        for s in range(NS):
                            ins = e.transpose(out=bv[:, hp, s * 128:(s + 1) * 128], in_=qrtok[s][:, hp * 128:(hp + 1) * 128],
                                              identity=identb[:])
                    return ins
                sc.op("pe", trr, reads=[("qrtok", s) for s in range(NS)], writes=[B(bi)])
                sc.op("act", lambda e, bv=bv: e.activation(out=qrT[:], in_=bv, func=AF.Copy), reads=[B(bi)], writes=["qrT"])

                chk("t%d_Q" % ti)
                nkt = (j + 1) * NS
                pti = [0]
                for h in range(H):
                    hp, hh = divmod(h, 2)
                    bo = bank("o")
                    for kt in range(nkt):
                        dd = kt - j * NS
                        q0 = max(dd, 0) * 128
                        N = T - q0
                        bs = bank("s")
                        kvreads = [("kvn", kt // NS, hp), ("kvr", kt // NS)]

                        def ms(e, h=h, hp=hp, hh=hh, kt=kt, q0=q0, N=N, bs=bs):
                            e.matmul(out=banks[bs][:, 0:N], lhsT=kTn[:, h, kt * 128:(kt + 1) * 128], rhs=qn[:, h, q0:T],
                                     start=True, stop=False)
                            return e.matmul(out=banks[bs][:, 0:N], lhsT=kTr[hh * 64:(hh + 1) * 64, kt * 128:(kt + 1) * 128],
                                            rhs=qrT[hh * 64:(hh + 1) * 64, hp, q0:T], start=False, stop=True)
                        sc.op("pe", ms, reads=kvreads + [("qn", hp), "qrT"], writes=[B(bs)])
                        pi = pti[0] % NPT
                        pti[0] += 1
                        sc.op("act", lambda e, pi=pi, bs=bs, N=N: e.activation(out=pT[pi][:, 0:N], in_=banks[bs][:, 0:N], func=AF.Exp,
                                                                                scale=ATTN_SCALE), reads=[B(bs)], writes=[("pT", pi)])
                        if dd >= 0:
                            sc.op("dve", lambda e, pi=pi: e.tensor_tensor(out=pT[pi][:, 0:128], in0=pT[pi][:, 0:128], in1=trib[:],
                                                                            op=ALU.mult), reads=[("pT", pi)], writes=[("pT", pi)])

                        def mpv(e, h=h, kt=kt, q0=q0, pi=pi, bo=bo):
                            ins = None
                            for qs in range(q0 // 128, NS):
                                ins = e.matmul(out=banks[bo][:, qs * 129:(qs + 1) * 129],
                                               lhsT=pT[pi][:, qs * 128 - q0:(qs + 1) * 128 - q0], rhs=Vaug[:, kt, h, :],
                                               start=(kt == 0 and qs == 0), stop=(kt == nkt - 1), skip_group_check=True)
                            return ins
                        sc.op("pe", mpv, reads=[("pT", pi), ("kvv", kt, h // 4), "vones"], writes=[B(bo)])
                    pv3 = banks[bo][:, 0:NS * 129].rearrange("p (q c) -> p q c", c=129)
                    sc.op("dve", lambda e, pv3=pv3: e.reciprocal(out=stat[:, 8:8 + NS], in_=pv3[:, :, 128]), reads=[B(bo)],
                          writes=[("stat", 8)])
                    for qs in range(NS):
                        sc.op("dve", lambda e, qs=qs, h=h, bo=bo: e.tensor_scalar(out=obf[:, qs, h * 128:(h + 1) * 128],
                                                                                   in0=banks[bo][:, qs * 129:qs * 129 + 128],
                                                                                   scalar1=stat[:, 8 + qs:9 + qs], scalar2=None,
                                                                                   op0=ALU.mult),
                              reads=[B(bo), ("stat", 8)], writes=[("obf", h)])

                chk("t%d_attn" % ti)
                for g in range(2):
                    bi = bank("t") if g == 0 else bank("g")
                    bv = banks[bi][:].bitcast(BF16).rearrange("p (a t) -> p a t", t=T)

                    def tro(e, g=g, bv=bv):
                        ins = None
                        for h4 in range(4):
                            hc = g * 4 + h4
                            for s in range(NS):
                                ins = e.transpose(out=bv[:, h4, s * 128:(s + 1) * 128], in_=obf[:, s, hc * 128:(hc + 1) * 128],
                                                  identity=identb[:])
                        return ins
                    sc.op("pe", tro, reads=[("obf", g * 4 + h4) for h4 in range(4)], writes=[B(bi)])
                    sc.op("dve", lambda e, g=g, bv=bv: e.scalar_tensor_tensor(out=y2T[:, g * 4:(g + 1) * 4, :], in0=bv, scalar=0.5,
                                                                               in1=sg[:, g * 4:(g + 1) * 4, :], op0=ALU.mult,
                                                                               op1=ALU.mult),
                          reads=[B(bi)] + [("sg", g * 4 + h4) for h4 in range(4)], writes=[("y2T", g)])
                for hf in range(2):
                    sl = acquire()
                    for s in range(NS):
                        bo = bank("o")

                        def mob(e, s=s, sl=sl, bo=bo):
                            ins = None
                            for hc in range(8):
                                ins = e.matmul(out=banks[bo][:, 0:512], lhsT=y2T[:, hc, s * 128:(s + 1) * 128],
                                               rhs=slots[sl][:, hc * 512:(hc + 1) * 512], start=(hc == 0), stop=(hc == 7))
                            return ins
                        sc.op("pe", mob, reads=[("slot", sl), ("y2T", 0), ("y2T", 1)], writes=[B(bo)])
                        sc.op("dve", lambda e, s=s, hf=hf, bo=bo: e.tensor_tensor(out=xt[:, s, hf * 512:(hf + 1) * 512],
                                                                                   in0=xt[:, s, hf * 512:(hf + 1) * 512],
                                                                                   in1=banks[bo][:, 0:512], op=ALU.add),
                              reads=[B(bo), ("xt", buf, s, hf)], writes=[("xt", buf, s, hf)])

                chk("t%d_oproj" % ti)
                xr = [[("xt", buf, s, 0), ("xt", buf, s, 1)] for s in range(NS)]
                rms_stats(xt, buf, 10, D, [xt[:, s, :] for s in range(NS)], xr, [hbf[s][:] for s in range(NS)],
                          [("hbf", s) for s in range(NS)])
                for s in range(NS):
                    sc.op("dve", lambda e, s=s: e.scalar_tensor_tensor(out=obuf[:], in0=xt[:, s, :], scalar=stat[:, 10 + s:11 + s],
                                                                        in1=gfin[:], op0=ALU.mult, op1=ALU.mult),
                          reads=xr[s] + [("stat", 10 + s), "gfin"], writes=["obuf"])
                    if os.environ.get("KDBG") == "xt":
                        sc.dma(lambda e, s=s: e.dma_start(out=out[r0 + s * 128:r0 + (s + 1) * 128, :], in_=xt[:, s, :]),
                               reads=xr[s], writes=[("out", ti, s)])
                    elif os.environ.get("KDBG") == "gfin":
                        sc.dma(lambda e, s=s: e.dma_start(out=out[r0 + s * 128:r0 + (s + 1) * 128, :], in_=gfin[:]),
                               reads=["gfin"], writes=[("out", ti, s)])
                    else:
                        sc.dma(lambda e, s=s: e.dma_start(out=out[r0 + s * 128:r0 + (s + 1) * 128, :], in_=obuf[:]),
                               reads=["obuf"], writes=[("out", ti, s)])


        run_block(emit_all, final=True)
    return nc


def host_consts(S):
    pos = np.arange(S, dtype=np.float32)
    inv = (np.float32(10000.0) ** (-np.arange(0, 64, 2, dtype=np.float32) / np.float32(64))).astype(np.float32)
    ang = (pos[:, None] * inv[None, :]).astype(np.float32)
    cos = np.cos(ang).astype(np.float32)
    sin = np.sin(ang).astype(np.float32)
    cs = np.concatenate([cos, cos, -sin, sin], axis=1).astype(np.float32)
    ident = np.eye(128, dtype=np.float32)
    tri = np.triu(np.ones((128, 128), dtype=np.float32))
    return cs, ident, tri


def make_in_maps(inputs, nseq, S, ncores):
    f = lambda a: np.ascontiguousarray(np.asarray(a, dtype=np.float32))
    col = lambda v: f(v).reshape(-1, 128).T
    vec = np.zeros((128, NV), np.float32)
    cw = f(inputs["conv_w"])[0]
    for k in range(4):
        vec[:, V_CW + k * 10:V_CW + (k + 1) * 10] = col(cw[k])
    vec[:, V_CB:V_CB + 10] = col(inputs["conv_b"][0])
    vec[:, V_BRG:V_BRG + 10] = col(inputs["b_rg"][0])
    vec[:, V_BIG:V_BIG + 10] = col(inputs["b_ig"][0])
    vec[:, V_LAM:V_LAM + 10] = col(inputs["lru_lambda"][0])
    vec[:, V_NA:V_NA + 8] = col(inputs["norm_a"][0])
    vec[:, V_NKV:V_NKV + 8] = col(inputs["norm_kv"])
    vec[:, V_NB:V_NB + 8] = col(inputs["norm_b"][0])
    vec[:, V_KVN:V_KVN + 2] = col(inputs["kv_norm"])
    vec[:, V_QN:V_QN + 3] = col(inputs["q_norm"][0])
    gfin = np.ascontiguousarray(np.broadcast_to(f(inputs["final_norm"])[None, :], (128, D)))
    cs, ident, tri = host_consts(S)
    shared = {
        "w_in_a": f(inputs["w_in_a"][0]), "w_rg": f(inputs["w_rg"][0]), "w_ig": f(inputs["w_ig"][0]),
        "w_out_a": f(inputs["w_out_a"][0]), "w_dkv": f(inputs["w_dkv"]),
        "w_uk": f(inputs["w_uk"]).reshape(KVR, 1024), "w_uv": f(inputs["w_uv"]).reshape(KVR, 1024),
        "w_in_b": f(inputs["w_in_b"][0]), "w_uq": f(inputs["w_uq"][0]), "w_out_b": f(inputs["w_out_b"][0]),
        "vec": vec, "gfin": gfin, "cs": cs, "ident": ident, "tri": tri,
    }
    xs = f(inputs["x"])
    maps = []
    for c in range(ncores):
        m = dict(shared)
        m["x"] = np.ascontiguousarray(xs[c * nseq:(c + 1) * nseq].reshape(nseq * S, D))
        maps.append(m)
    return maps


_NC_CACHE = {}


def kernel(**inputs):
    x = np.asarray(inputs["x"])
    Bsz, S, _ = x.shape
    nseq = Bsz // N_CORES
    key = (nseq, S)
    if key not in _NC_CACHE:
        _NC_CACHE[key] = build(nseq, S)
    nc = _NC_CACHE[key]
    maps = make_in_maps(inputs, nseq, S, N_CORES)
    res = run_bass_kernel_spmd(nc, maps, core_ids=list(range(N_CORES)))
    outs = [np.asarray(r["out"], dtype=np.float32).reshape(nseq, S, D) for r in res.results]
    return np.concatenate(outs, axis=0)
```
